# Optimizing a Trainium2 kernel written in Bass

```python
import math
import jax, jax.numpy as jnp
from jax import lax
import numpy as np


D_MODEL = 1024
BATCH = 16
SEQ = 2048
DEPTH = 1

D_MIX = D_MODEL
HEAD_DIM = 64
N_ATTN_HEADS = 8
N_KV_GROUPS = 2
HEADS_PER_GROUP = N_ATTN_HEADS // N_KV_GROUPS
D_ATTN = N_ATTN_HEADS * HEAD_DIM
D_KV = N_KV_GROUPS * HEAD_DIM
N_BRANCH = 3
D_CONV = D_MIX - D_ATTN
N_CONV_GROUPS = 8
CONV_WIDTH = 3
CMP_BLOCK = 32
CMP_STRIDE = 16
CMP_HIDDEN = 256
SEL_BLOCK = 64
SEL_TOPK = 16
N_LOCAL_FORCED = 2
WINDOW = 512
Q_CHUNK = 32
N_BUCKETS = 32
MAX_DISTANCE = 128
D_FF = -(-(8 * D_MODEL // 3) // 256) * 256
D_IN_PROJ = D_ATTN + 6 * D_KV + N_BRANCH * N_ATTN_HEADS + 3 * D_CONV
EPS = 1e-6
NEG_INF = -1e30
FORCED_SCORE = 1e6

kernel_name = "hybrid_nsa_shortconv_adaln_layer"


def rms_norm(x, gain):
    x32 = x.astype(jnp.float32)
    y = x32 * lax.rsqrt(jnp.mean(x32 * x32, axis=-1, keepdims=True) + EPS)
    return y.astype(x.dtype) * gain


def t5_bucket(rel):
    n = jnp.maximum(rel, 0)
    max_exact = N_BUCKETS // 2
    nf = jnp.maximum(n, 1).astype(jnp.float32)
    large = max_exact + (jnp.log(nf / max_exact) / math.log(MAX_DISTANCE / max_exact)
                         * (N_BUCKETS - max_exact)).astype(jnp.int32)
    return jnp.where(n < max_exact, n, jnp.minimum(large, N_BUCKETS - 1))


def head_bias(table, rel):
    b = table[t5_bucket(rel)]
    return jnp.moveaxis(b, -1, 0).reshape((N_KV_GROUPS, HEADS_PER_GROUP) + tuple(rel.shape))


def masked_softmax(logits, mask):
    logits = jnp.where(mask, logits.astype(jnp.float32), NEG_INF)
    p = jax.nn.softmax(logits, axis=-1)
    return p * jnp.any(mask, axis=-1, keepdims=True)


def nsa_mixer(q, k_c, v_c, k_s, v_s, k_w, v_w, gate_logits, q_gain, k_cmp_gain, k_sel_gain,
              k_win_gain, cmp_pos_k, cmp_pos_v, w_ck1, w_ck2, w_cv1, w_cv2, rel_bias_table):
    bsz, seq = q.shape[0], q.shape[1]
    G, HPG, DH = N_KV_GROUPS, HEADS_PER_GROUP, HEAD_DIM
    scale = DH ** -0.5
    t_pos = np.arange(seq)

    q = rms_norm(q.reshape(bsz, seq, G, HPG, DH), q_gain).transpose(0, 2, 3, 1, 4)

    def kv_heads(a):
        return a.reshape(bsz, seq, G, DH).transpose(0, 2, 1, 3)

    n_cmp = (seq - CMP_BLOCK) // CMP_STRIDE + 1
    cmp_start = np.arange(n_cmp) * CMP_STRIDE
    cmp_end = cmp_start + CMP_BLOCK - 1
    cmp_idx = cmp_start[:, None] + np.arange(CMP_BLOCK)[None, :]

    def compress(a, pos_emb, w1, w2):
        blocks = kv_heads(a)[:, :, cmp_idx] + pos_emb
        flat = blocks.reshape(bsz, G, n_cmp, CMP_BLOCK * DH)
        return jnp.dot(jax.nn.silu(jnp.dot(flat, w1)), w2)

    kc = rms_norm(compress(k_c, cmp_pos_k, w_ck1, w_ck2), k_cmp_gain)
    vc = compress(v_c, cmp_pos_v, w_cv1, w_cv2)
    rel_c = t_pos[:, None] - cmp_end[None, :]
    logits_c = jnp.einsum('bghsd,bgnd->bghsn', q, kc) * scale + head_bias(rel_bias_table, rel_c)
    p_c = masked_softmax(logits_c, jnp.asarray(rel_c >= 0))
    o_c = jnp.einsum('bghsn,bgnd->bghsd', p_c.astype(vc.dtype), vc)

    n_sel = seq // SEL_BLOCK
    top_k = min(SEL_TOPK, n_sel)
    sel_start = np.arange(n_sel) * SEL_BLOCK
    overlap = np.clip(np.minimum(cmp_end[:, None], sel_start[None, :] + SEL_BLOCK - 1)
                      - np.maximum(cmp_start[:, None], sel_start[None, :]) + 1, 0, None) / CMP_STRIDE
    p_slc = jnp.einsum('bghsn,nj->bgsj', p_c, jnp.asarray(overlap, jnp.float32))
    dist_blk = (t_pos // SEL_BLOCK)[:, None] - np.arange(n_sel)[None, :]
    valid = dist_blk >= 0
    forced = (np.arange(n_sel)[None, :] == 0) | (valid & (dist_blk < N_LOCAL_FORCED))
    score = jnp.where(valid, jnp.where(forced, FORCED_SCORE, p_slc), NEG_INF)
    sel_idx = lax.top_k(score, top_k)[1]

    k_blocks = rms_norm(kv_heads(k_s), k_sel_gain).reshape(bsz, G, n_sel, SEL_BLOCK, DH)
    v_blocks = kv_heads(v_s).reshape(bsz, G, n_sel, SEL_BLOCK, DH)
    tab_g = rel_bias_table.reshape(N_BUCKETS, G, HPG)
    b_ar = jnp.arange(bsz)[:, None, None, None]
    g_ar = jnp.arange(G)[None, :, None, None]
    n_tok = top_k * SEL_BLOCK

    pad = ((0, 0), (0, 0), (WINDOW, 0), (0, 0))
    kwin = jnp.pad(rms_norm(kv_heads(k_w), k_win_gain), pad)
    vwin = jnp.pad(kv_heads(v_w), pad)
    span = WINDOW + Q_CHUNK
    rel_w = WINDOW + np.arange(Q_CHUNK)[:, None] - np.arange(span)[None, :]
    bias_w = head_bias(rel_bias_table, rel_w)

    def chunk(ci):
        s0 = ci * Q_CHUNK
        t_q = s0 + jnp.arange(Q_CHUNK)
        qc = lax.dynamic_slice_in_dim(q, s0, Q_CHUNK, axis=3)
        idx = lax.dynamic_slice_in_dim(sel_idx, s0, Q_CHUNK, axis=2)
        kg = k_blocks[b_ar, g_ar, idx].reshape(bsz, G, Q_CHUNK, n_tok, DH)
        vg = v_blocks[b_ar, g_ar, idx].reshape(bsz, G, Q_CHUNK, n_tok, DH)
        key_pos = (idx[..., None] * SEL_BLOCK + jnp.arange(SEL_BLOCK)).reshape(bsz, G, Q_CHUNK, n_tok)
        rel_s = t_q[None, None, :, None] - key_pos
        bias_s = jnp.moveaxis(tab_g[t5_bucket(rel_s), g_ar], -1, 2)
        logits_s = jnp.einsum('bghqd,bgqkd->bghqk', qc, kg) * scale + bias_s
        p_s = masked_softmax(logits_s, (rel_s >= 0)[:, :, None])
        o_s = jnp.einsum('bghqk,bgqkd->bghqd', p_s.astype(vg.dtype), vg)
        kw = lax.dynamic_slice_in_dim(kwin, s0, span, axis=2)
        vw = lax.dynamic_slice_in_dim(vwin, s0, span, axis=2)
        key_pos_w = s0 - WINDOW + jnp.arange(span)
        rel = t_q[:, None] - key_pos_w[None, :]
        mask_w = (rel >= 0) & (rel < WINDOW) & (key_pos_w[None, :] >= 0)
        logits_w = jnp.einsum('bghqd,bgkd->bghqk', qc, kw) * scale + bias_w
        p_w = masked_softmax(logits_w, mask_w)
        o_w = jnp.einsum('bghqk,bgkd->bghqd', p_w.astype(vw.dtype), vw)
        return o_s, o_w

    o_s, o_w = lax.map(chunk, jnp.arange(seq // Q_CHUNK))

    def unchunk(o):
        return jnp.moveaxis(o, 0, 3).reshape(bsz, G, HPG, seq, DH)

    gates = jax.nn.sigmoid(gate_logits.reshape(bsz, seq, G, HPG, N_BRANCH)).transpose(0, 2, 3, 1, 4)
    o = gates[..., 0:1] * o_c + gates[..., 1:2] * unchunk(o_s) + gates[..., 2:3] * unchunk(o_w)
    return o.transpose(0, 3, 1, 2, 4).reshape(bsz, seq, D_ATTN)


def short_conv_mixer(b_gate, c_gate, xt, conv_w):
    seq = xt.shape[1]
    u = c_gate * xt
    u_pad = jnp.pad(u, ((0, 0), (CONV_WIDTH - 1, 0), (0, 0)))
    conv = sum(u_pad[:, k:k + seq] * conv_w[k] for k in range(CONV_WIDTH))
    return b_gate * conv


def swiglu(h, w1, w3, w2):
    return jnp.dot(jax.nn.silu(jnp.dot(h, w1)) * jnp.dot(h, w3), w2)


def setup_inputs(seed: int = 0) -> dict:
    key = jax.random.key(seed)
    ks = jax.random.split(key, 25)
    L = DEPTH

    def nrm(k, shape, s):
        return jax.random.normal(k, shape, jnp.float32) * s

    def gain(k, shape):
        return 1.0 + nrm(k, shape, 0.02)

    return {
        "x": nrm(ks[0], (BATCH, SEQ, D_MODEL), 1.0),
        "c": nrm(ks[1], (BATCH, D_MODEL), 1.0),
        "w_ada": nrm(ks[2], (L, D_MODEL, 6 * D_MODEL), 0.5 * D_MODEL ** -0.5),
        "b_ada": nrm(ks[3], (L, 6 * D_MODEL), 0.02),
        "norm1_gain": gain(ks[4], (L, D_MODEL)),
        "w_in": nrm(ks[5], (L, D_MODEL, D_IN_PROJ), D_MODEL ** -0.5),
        "q_gain": gain(ks[6], (L, HEAD_DIM)),
        "k_cmp_gain": gain(ks[7], (L, HEAD_DIM)),
        "k_sel_gain": gain(ks[8], (L, HEAD_DIM)),
        "k_win_gain": gain(ks[9], (L, HEAD_DIM)),
        "cmp_pos_k": nrm(ks[10], (L, CMP_BLOCK, HEAD_DIM), 0.1),
        "cmp_pos_v": nrm(ks[11], (L, CMP_BLOCK, HEAD_DIM), 0.1),
        "w_ck1": nrm(ks[12], (L, CMP_BLOCK * HEAD_DIM, CMP_HIDDEN), (CMP_BLOCK * HEAD_DIM) ** -0.5),
        "w_ck2": nrm(ks[13], (L, CMP_HIDDEN, HEAD_DIM), CMP_HIDDEN ** -0.5),
        "w_cv1": nrm(ks[14], (L, CMP_BLOCK * HEAD_DIM, CMP_HIDDEN), (CMP_BLOCK * HEAD_DIM) ** -0.5),
        "w_cv2": nrm(ks[15], (L, CMP_HIDDEN, HEAD_DIM), CMP_HIDDEN ** -0.5),
        "rel_bias_table": nrm(ks[16], (N_BUCKETS, N_ATTN_HEADS), 0.5),
        "conv_w": nrm(ks[17], (L, CONV_WIDTH, D_CONV), CONV_WIDTH ** -0.5),
        "attn_out_gain": gain(ks[18], (L, D_ATTN)),
        "conv_out_gain": gain(ks[19], (L, D_CONV)),
        "w_out": nrm(ks[20], (L, D_MIX, D_MODEL), D_MIX ** -0.5),
        "norm2_gain": gain(ks[21], (L, D_MODEL)),
        "w_ff1": nrm(ks[22], (L, D_MODEL, D_FF), D_MODEL ** -0.5),
        "w_ff3": nrm(ks[23], (L, D_MODEL, D_FF), D_MODEL ** -0.5),
        "w_ff2": nrm(ks[24], (L, D_FF, D_MODEL), D_FF ** -0.5),
    }


def reference(x, c, w_ada, b_ada, norm1_gain, w_in, q_gain, k_cmp_gain, k_sel_gain, k_win_gain,
              cmp_pos_k, cmp_pos_v, w_ck1, w_ck2, w_cv1, w_cv2, rel_bias_table, conv_w,
              attn_out_gain, conv_out_gain, w_out, norm2_gain, w_ff1, w_ff3, w_ff2):
    split_points = np.cumsum([D_ATTN] + [D_KV] * 6 + [N_BRANCH * N_ATTN_HEADS] + [D_CONV] * 3)[:-1].tolist()
    for layer in range(DEPTH):
        mod = jnp.dot(jax.nn.silu(c), w_ada[layer]) + b_ada[layer]
        shift1, scale1, gate1, shift2, scale2, gate2 = jnp.split(mod[:, None, :], 6, axis=-1)

        h = rms_norm(x, norm1_gain[layer]) * (1 + scale1) + shift1
        proj = jnp.dot(h, w_in[layer])
        q, k_c, v_c, k_s, v_s, k_w, v_w, gate_logits, b_gate, c_gate, xt = jnp.split(proj, split_points, axis=-1)
        y_attn = nsa_mixer(q, k_c, v_c, k_s, v_s, k_w, v_w, gate_logits, q_gain[layer],
                           k_cmp_gain[layer], k_sel_gain[layer], k_win_gain[layer],
                           cmp_pos_k[layer], cmp_pos_v[layer], w_ck1[layer], w_ck2[layer],
                           w_cv1[layer], w_cv2[layer], rel_bias_table)
        y_conv = short_conv_mixer(b_gate, c_gate, xt, conv_w[layer])
        y = jnp.concatenate([rms_norm(y_attn, attn_out_gain[layer]),
                             rms_norm(y_conv, conv_out_gain[layer])], axis=-1)
        x = x + gate1 * jnp.dot(y, w_out[layer])

        h2 = rms_norm(x, norm2_gain[layer]) * (1 + scale2) + shift2
        x = x + gate2 * swiglu(h2, w_ff1[layer], w_ff3[layer], w_ff2[layer])
    return x
```

```python
import numpy as np
from contextlib import ExitStack
import concourse.bass as bass
import concourse.mybir as mybir
from concourse.bass_utils import run_bass_kernel_spmd

F32 = mybir.dt.float32
BF16 = mybir.dt.bfloat16
ALU = mybir.AluOpType
AF = mybir.ActivationFunctionType
AX = mybir.AxisListType

ENGS = ("pe", "act", "dve", "pool", "sp")


class T:
    __slots__ = ("ap", "keys")

    def __init__(self, ap, keys):
        self.ap = ap
        self.keys = (keys,) if isinstance(keys, str) else tuple(keys)

    def __getitem__(self, idx):
        return T(self.ap[idx], self.keys)

    def k(self, *keys):
        return T(self.ap, keys)

    def bc(self, shape):
        return T(self.ap.to_broadcast(list(shape)), self.keys)

    def re(self, s, **kw):
        return T(self.ap.rearrange(s, **kw), self.keys)

    def cast(self, dt):
        return T(self.ap.bitcast(dt), self.keys)


class Op:
    __slots__ = ("eng", "fn", "deps", "idx", "sig", "waits", "dsem", "dval", "ev")


class Sched:
    def __init__(self, nc, es):
        self.nc = nc
        self.es = es
        self.ops = {e: [] for e in ENGS}
        self.order = []
        self.res = {}
        self.dsems = {}
        self.esem = {}
        self.last_dma_ev = {}

    def _deps(self, eng, reads, writes):
        d = {}

        def add(ev):
            if ev is None:
                return
            cid, idx = ev
            if cid == "pe" and eng == "pe":
                return
            if d.get(cid, -1) < idx:
                d[cid] = idx
        for r in reads:
            st = self.res.get(r)
            if st is not None:
                add(st[0])
        for w in writes:
            st = self.res.get(w)
            if st is not None:
                add(st[0])
                for ev in st[1].items():
                    add(ev)
        return d

    def _commit(self, ev, reads, writes):
        for r in reads:
            st = self.res.get(r)
            if st is None:
                self.res[r] = [None, {ev[0]: ev[1]}]
            elif st[1].get(ev[0], -1) < ev[1]:
                st[1][ev[0]] = ev[1]
        for w in writes:
            self.res[w] = [ev, {}]

    @staticmethod
    def _keys(ts):
        out = []
        for t in ts:
            if t is None or isinstance(t, (int, float)):
                continue
            out.extend(t.keys)
        return out

    def emit(self, eng, fn, reads, writes):
        rk = self._keys(reads)
        wk = self._keys(writes)
        op = Op()
        op.eng = eng
        op.fn = fn
        op.deps = self._deps(eng, rk, wk)
        op.idx = len(self.ops[eng])
        op.sig = False
        op.dsem = None
        op.dval = 0
        op.ev = (eng, op.idx)
        self.ops[eng].append(op)
        self.order.append(op)
        self._commit(op.ev, rk, wk)
        return op

    def dma(self, queue, out, in_, sem, **kw):
        rk = self._keys([in_])
        wk = self._keys([out])
        if sem not in self.dsems:
            self.dsems[sem] = [self.es.enter_context(self.nc.semaphore("d_" + sem)), 0]
        st = self.dsems[sem]
        st[1] += 16
        op = Op()
        op.eng = queue
        oa, ia = out.ap, in_.ap
        op.fn = lambda e: e.dma_start(out=oa, in_=ia, **kw)
        op.deps = self._deps(queue, rk, wk)
        op.idx = len(self.ops[queue])
        op.sig = False
        op.dsem = sem
        op.dval = st[1]
        op.ev = ("d:" + sem, st[1])
        self.ops[queue].append(op)
        self.order.append(op)
        self._commit(op.ev, rk, wk)
        self.last_dma_ev[sem] = op.ev
        return op

    def barrier(self):
        evs = {}
        for e in ("pe", "act", "dve", "pool"):
            for op in reversed(self.ops[e]):
                if op.fn is not None and op.dsem is None:
                    evs[e] = op.idx
                    break
        for ev in self.last_dma_ev.values():
            evs[ev[0]] = ev[1]
        for e in ENGS:
            op = Op()
            op.eng = e
            op.fn = None
            op.deps = {c: i for c, i in evs.items() if not (c == e and e == "pe")}
            op.idx = len(self.ops[e])
            op.sig = False
            op.dsem = None
            op.dval = 0
            op.ev = (e, op.idx)
            self.ops[e].append(op)
            self.order.append(op)

    def wait_all_dma(self, eng="sp"):
        op = Op()
        op.eng = eng
        op.fn = None
        op.deps = {ev[0]: ev[1] for ev in self.last_dma_ev.values()}
        op.idx = len(self.ops[eng])
        op.sig = False
        op.dsem = None
        op.dval = 0
        op.ev = (eng, op.idx)
        self.ops[eng].append(op)
        self.order.append(op)

    def finalize(self):
        known = {e: {} for e in ENGS}
        clocks = {}
        for op in self.order:
            kn = known[op.eng]
            waits = []
            for cid, idx in op.deps.items():
                if kn.get(cid, -1) >= idx:
                    continue
                waits.append((cid, idx))
                ck = clocks.get((cid, idx))
                if ck is not None:
                    for c2, i2 in ck.items():
                        if kn.get(c2, -1) < i2:
                            kn[c2] = i2
                if kn.get(cid, -1) < idx:
                    kn[cid] = idx
                if not cid.startswith("d:"):
                    tg = self.ops[cid][idx]
                    assert tg.fn is not None and tg.dsem is None, (cid, idx)
                    tg.sig = True
            op.waits = waits
            ck = dict(kn)
            ck[op.ev[0]] = max(ck.get(op.ev[0], -1), op.ev[1])
            clocks[op.ev] = ck
        self.semval = {}
        for e in ENGS:
            n = 0
            vals = []
            for op in self.ops[e]:
                if op.fn is None:
                    op.sig = False
                if op.sig:
                    n += 1
                vals.append(n)
            self.semval[e] = vals
        for e in ENGS:
            for op in self.ops[e]:
                if op.fn is None:
                    pass

    def replay(self):
        nc = self.nc
        for e in ENGS:
            self.esem[e] = self.es.enter_context(nc.semaphore("e_" + e))
        block = self.es.enter_context(nc.Block())

        def body(e):
            def run(eng):
                for op in self.ops[e]:
                    for cid, idx in op.waits:
                        if cid.startswith("d:"):
                            eng.wait_ge(self.dsems[cid[2:]][0], idx)
                        else:
                            eng.wait_ge(self.esem[cid], self.semval[cid][idx])
                    if op.fn is None:
                        continue
                    ins = op.fn(eng)
                    if op.dsem is not None:
                        ins.then_inc(self.dsems[op.dsem][0], 16)
                    elif op.sig:
                        ins.then_inc(self.esem[e], 1)
            return run
        block.tensor(body("pe"))
        block.scalar(body("act"))
        block.vector(body("dve"))
        block.gpsimd(body("pool"))
        block.sync(body("sp"))

    def mm(self, out, lhsT, rhs, start=True, stop=True, skip=False):
        o, l, r = out.ap, lhsT.ap, rhs.ap
        reads = [lhsT, rhs]
        if skip:
            return self.emit("pe", lambda e: e.matmul(o, l, r, start=start, stop=stop, skip_group_check=True), reads, [out])
        return self.emit("pe", lambda e: e.matmul(o, l, r, start=start, stop=stop), reads, [out])

    def tr(self, out, in_, ident):
        o, i, d = out.ap, in_.ap, ident.ap
        return self.emit("pe", lambda e: e.transpose(o, i, d), [in_, ident], [out])

    def act(self, out, in_, func, bias=0.0, scale=1.0, accum=None):
        o, i = out.ap, in_.ap
        b = bias.ap if isinstance(bias, T) else bias
        s = scale.ap if isinstance(scale, T) else scale
        kw = {}
        if accum is not None:
            kw["accum_out"] = accum.ap
        return self.emit("act", lambda e: e.activation(o, i, func, bias=b, scale=s, **kw),
                         [in_, bias if isinstance(bias, T) else None, scale if isinstance(scale, T) else None],
                         [out, accum])

    def ts(self, out, in0, s1, s2=None, op0=ALU.mult, op1=None, eng="dve", accum=None):
        o, i = out.ap, in0.ap
        a = s1.ap if isinstance(s1, T) else s1
        b = s2.ap if isinstance(s2, T) else s2
        kw = {}
        if op1 is not None:
            kw["op1"] = op1
        if accum is not None:
            kw["accum_out"] = accum.ap
        return self.emit(eng, lambda e: e.tensor_scalar(o, i, a, b, op0, **kw),
                         [in0, s1 if isinstance(s1, T) else None, s2 if isinstance(s2, T) else None],
                         [out, accum])

    def tt(self, out, in0, in1, op, eng="dve"):
        o, a, b = out.ap, in0.ap, in1.ap
        return self.emit(eng, lambda e: e.tensor_tensor(o, a, b, op), [in0, in1], [out])

    def stt(self, out, in0, scalar, in1, op0, op1, eng="dve"):
        o, a, b = out.ap, in0.ap, in1.ap
        s = scalar.ap if isinstance(scalar, T) else scalar
        return self.emit(eng, lambda e: e.scalar_tensor_tensor(o, a, s, b, op0, op1),
                         [in0, in1, scalar if isinstance(scalar, T) else None], [out])

    def copy(self, out, in_, eng="dve"):
        o, i = out.ap, in_.ap
        if eng == "act":
            return self.emit("act", lambda e: e.copy(o, i), [in_], [out])
        return self.emit(eng, lambda e: e.tensor_copy(o, i), [in_], [out])

    def memset(self, out, val, eng="dve"):
        o = out.ap
        return self.emit(eng, lambda e: e.memset(o, val), [], [out])

    def reduce(self, out, in_, op=ALU.add, axis=AX.X, eng="dve"):
        o, i = out.ap, in_.ap
        return self.emit(eng, lambda e: e.tensor_reduce(o, i, axis, op), [in_], [out])

    def recip(self, out, in_):
        o, i = out.ap, in_.ap
        return self.emit("dve", lambda e: e.reciprocal(o, i), [in_], [out])

    def max8(self, out, in_):
        o, i = out.ap, in_.ap
        return self.emit("dve", lambda e: e.max(o, i), [in_], [out])

    def match_replace(self, out, to_replace, values, imm):
        o, r, v = out.ap, to_replace.ap, values.ap
        return self.emit("dve", lambda e: e.match_replace(o, r, v, imm), [to_replace, values], [out])


import math

S_LEN = 2048
DM = 1024
NT = 16
DIN = 2840
DFF = 2816
NF = 22
BIG = 30000.0
EPS = 1e-6
DEBUG = False
DBG_B = 0


class Mem:
    def __init__(self, nc, es, nwords):
        self.h = es.enter_context(nc.sbuf_tensor("arena", [128, nwords], F32))
        self.n = nwords
        self.bufs = []

    def alloc(self, name, free_shape, dt, phases, parts=128):
        if isinstance(free_shape, int):
            free_shape = [free_shape]
        nel = int(np.prod(free_shape))
        nfl = nel if dt == F32 else (nel + 1) // 2
        nfl = (nfl + 7) // 8 * 8
        phases = set(phases)
        cands = sorted([0] + [e for (s, e, p) in self.bufs])
        off = None
        for c in cands:
            ok = c + nfl <= self.n
            if ok:
                for (s, e, p) in self.bufs:
                    if p & phases and not (c + nfl <= s or c >= e):
                        ok = False
                        break
            if ok:
                off = c
                break
        assert off is not None, ("SBUF alloc failed", name, nfl)
        self.bufs.append((off, off + nfl, phases))
        ap = self.h[:, off:off + nfl]
        if dt != F32:
            ap = ap.bitcast(dt)
        ap = ap[:, 0:nel]
        if len(free_shape) == 2:
            ap = ap.rearrange("p (a b) -> p a b", b=free_shape[1])
        elif len(free_shape) == 3:
            ap = ap.rearrange("p (a b c) -> p a b c", b=free_shape[1], c=free_shape[2])
        if parts != 128:
            ap = ap[0:parts]
        return T(ap, name)


def _t5_bucket(rel):
    n = np.maximum(rel, 0)
    nf = np.maximum(n, 1).astype(np.float32)
    large = 16 + (np.log(nf / np.float32(16)) / np.float32(math.log(8)) * np.float32(16)).astype(np.int32)
    return np.where(n < 16, n, np.minimum(large, 31))


_CPK = {}
_off = 0
for _n, _w in [("identf", 128), ("b31", 8), ("qg", 64), ("ksg", 64), ("kwg", 64), ("kcg", 64), ("cog", 4), ("cw", 12),
               ("g1", 8), ("g2", 8), ("cT", 16), ("posk", 16), ("posv", 16)]:
    _CPK[_n] = (_off, _w)
    _off += _w
NCPKA = _off
for _n, _w in [("mask01", 256), ("mfar", 128), ("M1", 512), ("Aadd", 512), ("aog", 512), ("validc", 512), ("SHM", 224)]:
    _CPK[_n] = (_off, _w)
    _off += _w
NCPK = _off


def _host_constants(inp, core):
    tab = np.asarray(inp["rel_bias_table"], np.float32)
    cpk = np.zeros((128, NCPK), np.float32)

    def put(name, arr):
        o, w = _CPK[name]
        arr = np.asarray(arr, np.float32).reshape(arr.shape[0], -1)
        assert arr.shape[1] == w, (name, arr.shape, w)
        cpk[:arr.shape[0], o:o + w] = arr
    kl = np.arange(128)[:, None]
    put("identf", np.eye(128, dtype=np.float32))
    put("mask01", (np.arange(256)[None, :] >= kl).astype(np.float32))
    put("mfar", (np.arange(128)[None, :] < kl).astype(np.float32))
    t_l = np.arange(128)[:, None, None]
    qt = np.arange(16)[None, :, None]
    j = np.arange(32)[None, None, :]
    bq = (128 * qt + t_l) // 64
    valid = j <= bq
    forced = (j == 0) | (valid & (bq - j < 2))
    m1 = (valid & ~forced).astype(np.float32)
    aadd = np.where(forced, np.where(j == 0, 1e6, np.where(j == bq, 2e6, 3e6)), 0.0)
    aadd = np.where(valid, aadd, -(1.0 + j))
    put("M1", m1.reshape(128, -1))
    put("Aadd", aadd.reshape(128, -1))
    put("b31", np.broadcast_to(tab[31][None, :], (128, 8)))
    L = 0
    put("qg", np.broadcast_to(inp["q_gain"][L][None, :], (128, 64)))
    put("ksg", np.broadcast_to(inp["k_sel_gain"][L][None, :], (128, 64)))
    put("kwg", np.broadcast_to(inp["k_win_gain"][L][None, :], (128, 64)))
    put("kcg", np.broadcast_to(inp["k_cmp_gain"][L][None, :], (128, 64)))
    put("aog", np.broadcast_to(inp["attn_out_gain"][L][None, :], (128, 512)))
    put("cog", inp["conv_out_gain"][L].reshape(4, 128).T)
    put("cw", inp["conv_w"][L].reshape(3, 4, 128).transpose(2, 1, 0).reshape(128, 12))
    put("g1", inp["norm1_gain"][L].reshape(8, 128).T)
    put("g2", inp["norm2_gain"][L].reshape(8, 128).T)
    cc = np.asarray(inp["c"], np.float32)[2 * core:2 * core + 2]
    put("cT", cc.reshape(2, 8, 128).transpose(2, 1, 0).reshape(128, 16))
    r = np.arange(40)[:, None]
    tl5 = np.arange(512)[None, :]
    relc = tl5 - 16 * (r - 8) - 31
    validc = ((relc >= 0) & (r < 39)).astype(np.float32)
    put("validc", validc)
    shm = np.zeros((40, 224), np.float32)
    for rr in range(39):
        shm[rr, rr + 88] = 1.0
    shm[39, 127:] = 1.0
    put("SHM", shm)
    put("posk", inp["cmp_pos_k"][L].reshape(16, 128).T)
    put("posv", inp["cmp_pos_v"][L].reshape(16, 128).T)
    relnd = np.maximum(np.arange(256)[None, :] - kl, 0)
    biasg = tab[_t5_bucket(relnd)]
    biasg = np.ascontiguousarray(biasg.transpose(0, 2, 1)).reshape(128, 8 * 256)
    patg = tab[_t5_bucket(np.maximum(relc, 0))]
    patg = np.ascontiguousarray(patg.transpose(0, 2, 1)).reshape(40, 8 * 512)
    bada = np.ascontiguousarray(np.broadcast_to(np.asarray(inp["b_ada"], np.float32)[L][None, :], (2, 6144)))
    kk = np.arange(2048)[None, :]
    eaug = (kk // 64 == np.arange(32)[:, None]).astype(np.float32)
    cs = np.arange(127) * 16
    ce = cs + 31
    ss_ = np.arange(32) * 64
    ovl = (np.clip(np.minimum(ce[:, None], ss_[None, :] + 63) - np.maximum(cs[:, None], ss_[None, :]) + 1, 0, None)
           / 16.0).astype(np.float32)
    sel2 = np.zeros((2, 2, 128), np.float32)
    return dict(cpk=cpk, biasg=biasg.astype(np.float32), patg=patg.astype(np.float32), bada=bada,
                eaug=eaug, ovl=ovl)


def build_program(dbg=None):
    nc = bass.Bass("TRN2", target_bir_lowering=False)

    def din(name, shape):
        return nc.dram_tensor(name, list(shape), F32, kind="ExternalInput").ap()
    x_d = din("x", [2, S_LEN, DM])
    w_ada_d = din("w_ada", [DM, 6 * DM])
    w_in_d = din("w_in", [DM, DIN])
    w_ck1_d = din("w_ck1", [2048, 256])
    w_ck2_d = din("w_ck2", [256, 64])
    w_cv1_d = din("w_cv1", [2048, 256])
    w_cv2_d = din("w_cv2", [256, 64])
    w_out_d = din("w_out", [DM, DM])
    w_ff1_d = din("w_ff1", [DM, DFF])
    w_ff3_d = din("w_ff3", [DM, DFF])
    w_ff2_d = din("w_ff2", [DFF, DM])
    cpk_d = din("cpk", [128, NCPK])
    biasg_d = din("biasg", [128, 2048])
    patg_d = din("patg", [40, 4096])
    bada_d = din("bada", [2, 6144])
    eaug_d = din("eaug", [32, 2048])
    ovl_d = din("ovl", [127, 32])
    out_d = nc.dram_tensor("out", [2, S_LEN, DM], F32, kind="ExternalOutput").ap()
    mod_d = nc.dram_tensor("mod_s", [2, 6 * DM], F32).ap()
    x1_d = nc.dram_tensor("x1_s", [2, S_LEN, DM], F32).ap()
    dbg_d = {}
    if dbg:
        for name, shape in dbg.items():
            dbg_d[name] = nc.dram_tensor("dbg_" + name, list(shape), F32, kind="ExternalOutput").ap()

    es = ExitStack()
    with es:
        S = Sched(nc, es)
        M = Mem(nc, es, 52800)
        ctx_lp = es.enter_context(nc.allow_low_precision("bf16 matmul operands, fp32 accumulation"))
        ctx_nc = es.enter_context(nc.allow_non_contiguous_dma("small strided parameter loads"))
        banks = [T(es.enter_context(nc.psum_tensor("bank%d" % i, [128, 512], F32))[:], "bank%d" % i) for i in range(8)]
        ALLP = {"P0", "P1", "P2s", "P2", "P3a", "P3b"}

        def DT(ap, key):
            return T(ap, key)

        cpk = M.alloc("cpk", NCPKA, F32, ALLP - {"P3b"})
        cpkB = M.alloc("cpkB", NCPK - NCPKA, F32, {"P2s", "P2"})

        def C(name):
            o, w = _CPK[name]
            if o >= NCPKA:
                return cpkB[:, o - NCPKA:o - NCPKA + w]
            return cpk[:, o:o + w]
        identb = M.alloc("identb", 128, BF16, ALLP)
        ones_bf = M.alloc("ones_bf", 128, BF16, ALLP)
        modc = M.alloc("modc", [4, 8], F32, ALLP)
        g2c = M.alloc("g2c", 8, F32, ALLP)
        stat0 = M.alloc("stat", 64, F32, ALLP)

        class _Stat:
            def __getitem__(self, idx):
                cs = idx[1]
                lo = cs.start
                blk = "A" if lo < 8 else "B" if lo < 32 else "C" if lo < 40 else "D" if lo < 48 else "E"
                return T(stat0.ap[idx], "stat" + blk)
        stat = _Stat()
        S.dma("sp", cpk, DT(cpk_d[:, 0:NCPKA], "in:cpk"), "cpk")
        S.copy(identb, C("identf"))
        S.memset(ones_bf, 1.0)
        S.copy(g2c, C("g2"))
        qgs = M.alloc("qgs", 64, F32, ALLP)
        S.ts(qgs, C("qg"), 0.125, None, op0=ALU.mult)

        def dump(name, t, idx=None):
            if name in dbg_d:
                dst = dbg_d[name] if idx is None else dbg_d[name][idx]
                S.dma("pool", DT(dst, "dbg:" + name), t, "dbg")

        scT = M.alloc("scT", 16, F32, {"P0"})
        etmp = M.alloc("etmp", 16, F32, {"P0"})
        mod_sb = M.alloc("mod_sb", 6144, F32, {"P0"}, parts=2)
        bada_sb = M.alloc("bada_sb", 6144, F32, {"P0"}, parts=2)
        wa = [M.alloc("wa%d" % i, [8, 512], F32, {"P0"}) for i in range(2)]
        S.dma("sp", bada_sb, DT(bada_d, "in:bada"), "bada")
        cT = C("cT")
        S.act(etmp, cT, AF.Exp, scale=-1.0)
        S.ts(etmp, etmp, 1.0, None, op0=ALU.add)
        S.recip(etmp, etmp)
        S.tt(scT, cT, etmp, ALU.mult)
        scT3 = scT.re("p (k b) -> p k b", b=2)
        w_ada_v = w_ada_d.rearrange("(c p) n -> p c n", p=128)
        for nb in range(12):
            wt = wa[nb % 2]
            S.dma("sp", wt, DT(w_ada_v[:, :, nb * 512:(nb + 1) * 512], "in:w_ada"), "wa%d" % (nb % 2))
            bk = banks[nb % 2]
            for kc in range(8):
                S.mm(bk[0:2, :], scT3[:, kc, :], wt[:, kc, :], start=(kc == 0), stop=(kc == 7))
            S.tt(mod_sb[:, nb * 512:(nb + 1) * 512], bk[0:2, :], bada_sb[:, nb * 512:(nb + 1) * 512], ALU.add)
        S.dma("sp", DT(mod_d, "dram:mod"), mod_sb, "modst")
        dump("mod", mod_sb)
        S.barrier()

        AT = {"P1", "P2s", "P2"}
        y_convT = M.alloc("y_convT", [4, S_LEN], BF16, {"P1", "P2s", "P2", "P3a"})
        y_attnT = M.alloc("y_attnT", [4, S_LEN], BF16, {"P2", "P3a"})
        w_out = M.alloc("w_out", [8, DM], BF16, {"P2", "P3a"})
        qT_all = M.alloc("qT_all", [8, S_LEN], BF16, AT)
        kT_s = M.alloc("kT_s", [2, S_LEN], BF16, AT)
        kT_w = M.alloc("kT_w", [2, S_LEN], BF16, AT)
        vtok_s = M.alloc("vtok_s", [NT, 2, 65], BF16, AT)
        vtok_w = M.alloc("vtok_w", [NT, 2, 65], BF16, AT)
        kcT2 = M.alloc("kcT2", [4, S_LEN], BF16, {"P1", "P2s"})
        gates = M.alloc("gates", [NT, 24], F32, AT)
        w_in = M.alloc("w_in", [8, DIN], BF16, {"P1"})
        xs = [M.alloc("xs%d" % i, DM, F32, {"P1", "P3a"}) for i in range(2)]
        sqscr = M.alloc("sqscr", DM, F32, {"P1"})
        xn = [M.alloc("xn%d" % i, DM, BF16, {"P1"}) for i in range(2)]
        hT = [M.alloc("hT%d" % i, [8, 512], BF16, {"P1"}) for i in range(2)]
        tmpq = M.alloc("tmpq", [8, 64], F32, {"P1"})
        qtok = M.alloc("qtok", [8, 64], BF16, {"P1"})
        tmpk = M.alloc("tmpk", [4, 64], F32, {"P1"})
        ktok = M.alloc("ktok", [4, 64], BF16, {"P1"})
        kc2 = M.alloc("kc2", [4, 128], BF16, {"P1"})
        gtmp = M.alloc("gtmp", 24, F32, {"P1"})
        xts = M.alloc("xts", 512, F32, {"P1"})
        uext = [M.alloc("uext%d" % i, 514, F32, {"P1"}) for i in range(4)]
        ctmp = M.alloc("ctmp", 512, F32, {"P1"})
        ctmp2 = M.alloc("ctmp2", 512, F32, {"P1"})
        ybuf = [M.alloc("ybuf%d" % i, 512, F32, {"P1"}) for i in range(4)]
        ysq = M.alloc("ysq", 512, BF16, {"P1"})
        rcv = M.alloc("rcv", 512, F32, {"P1"})
        modl = M.alloc("modl", [2, 8], F32, {"P1", "P3b"})

        EWn = M.alloc("EWn", [8, 256], F32, {"P2s", "P2"})
        biasg = M.alloc("biasg", [8, 256], F32, {"P2s"})
        Pat = M.alloc("Pat", [8, 512], F32, {"P2s", "P2"}, parts=40)
        patg = M.alloc("patg", [8, 512], F32, {"P2s"}, parts=40)
        negb = M.alloc("negb", 512, F32, {"P2s"}, parts=40)
        nb31 = M.alloc("nb31", 8, F32, {"P2s"})
        w_c1 = [M.alloc("w_c1_%d" % i, [16, 256], BF16, {"P2s"}) for i in range(2)]
        w_c2 = [M.alloc("w_c2_%d" % i, [2, 64], BF16, {"P2s"}) for i in range(2)]
        posb = M.alloc("posb", [2, 16], BF16, {"P2s"})
        posbias = M.alloc("posbias", [2, 2], F32, {"P2s"})
        hidT = M.alloc("hidT", [2, 128], BF16, {"P2s"})
        kcn = M.alloc("kcn", 64, BF16, {"P2s"})
        kcT = M.alloc("kcT", [2, 128], BF16, {"P2s", "P2"})
        vc_aug = M.alloc("vc_aug", [2, 97], BF16, {"P2s", "P2"})
        PcT = [M.alloc("PcT%d" % i, 512, BF16, {"P2"}) for i in range(2)]
        PT = [M.alloc("PT%d" % i, 512, BF16, {"P2"}) for i in range(4)]
        yacc = M.alloc("yacc", [4, 512], F32, {"P2"})
        ysq2 = M.alloc("ysq2", 512, F32, {"P2"})
        pslc = M.alloc("pslc", [4, 2, 32], F32, {"P2"})
        ptmp = M.alloc("ptmp", [4, 32], F32, {"P2"})
        otmp = M.alloc("otmp", [4, 64], F32, {"P2"})
        sc_a = M.alloc("sc_a", 32, F32, {"P2"})
        sc_b = M.alloc("sc_b", 32, F32, {"P2"})
        m8 = M.alloc("m8", 16, F32, {"P2"})
        nm96 = M.alloc("nm96", 96, BF16, {"P2s", "P2"})
        sqc = M.alloc("sqc", 64, F32, {"P2s"})
        yn = M.alloc("yn", [4, 512], BF16, {"P2"})
        w_ff1 = M.alloc("w_ff1", [8, DFF], BF16, {"P3a", "P3b"})
        w_ff3 = M.alloc("w_ff3", [8, DFF], BF16, {"P3a", "P3b"})
        w_ff2 = M.alloc("w_ff2", [NF, DM], BF16, {"P3a", "P3b"})
        gbc = M.alloc("gbc", DM, F32, {"P3a", "P3b"})
        x1t = [M.alloc("x1t%d" % i, DM, F32, {"P3a"}) for i in range(2)]
        x1g = M.alloc("x1g", [4, DM], F32, {"P3b"})
        xn3 = [M.alloc("xn3_%d" % i, DM, BF16, {"P3b"}) for i in range(2)]
        sq3 = M.alloc("sq3", DM, F32, {"P3b"})
        h2T = M.alloc("h2T", [8, 512], BF16, {"P3b"})
        uT = M.alloc("uT", [NF, 512], BF16, {"P3b"})
        sg = M.alloc("sg", 512, F32, {"P3b"})
        ot = [M.alloc("ot%d" % i, DM, F32, {"P3b"}) for i in range(2)]

        def bc_last(t, n):
            sh = list(t.ap.shape)
            return T(t.ap.unsqueeze(len(sh)).to_broadcast(sh + [n]), t.keys)

        def bc_mid(t, n):
            sh = list(t.ap.shape)
            return T(t.ap.unsqueeze(1).to_broadcast([sh[0], n] + sh[1:]), t.keys)

        def rstd_from(out, ss_in, n):
            S.act(out, ss_in, AF.Ln, scale=1.0 / n, bias=EPS)
            S.act(out, out, AF.Exp, scale=-0.5)

        def wview(w_d):
            return w_d.rearrange("(c p) n -> p c n", p=128)

        for b in range(2):
            if b > 0:
                S.dma("sp", cpk, DT(cpk_d[:, 0:NCPKA], "in:cpk"), "cpk")
            S.dma("pool", w_in[:, 0:4, :], DT(wview(w_in_d)[:, 0:4, :], "in:w_in"), "win0")
            S.dma("pool", w_in[:, 4:8, :], DT(wview(w_in_d)[:, 4:8, :], "in:w_in"), "win1")
            S.dma("sp", modl[:, 0, :], DT(mod_d[b, DM:2 * DM].rearrange("(c p) -> p c", p=128), "dram:mod"), "modl0")
            S.dma("sp", modl[:, 1, :], DT(mod_d[b, 0:DM].rearrange("(c p) -> p c", p=128), "dram:mod"), "modl1")
            S.stt(modc[:, 0, :], modl[:, 0, :], 1.0, C("g1"), ALU.add, ALU.mult)
            S.copy(modc[:, 1, :], modl[:, 1, :])
            for g in range(2):
                S.dma("pool", kT_s[64:96, g, :], DT(eaug_d, "in:eaug"), "eaug%d" % g)
            S.memset(vtok_s[:, :, :, 64:65], 1.0)
            S.memset(vtok_w[:, :, :, 64:65], 1.0)
            for cc in range(4):
                S.memset(uext[cc][:, 0:2], 0.0)

            for gi in range(4):
                hTg = hT[gi % 2]
                for tl in range(4):
                    ti = gi * 4 + tl
                    xt = xs[ti % 2]
                    S.dma("sp", xt, DT(x_d[b, ti * 128:(ti + 1) * 128, :], "in:x"), "x%d" % (ti % 2))
                    S.act(sqscr, xt, AF.Square, accum=stat[:, 0:1])
                    rstd_from(stat[:, 1:2], stat[:, 0:1], DM)
                    xnt = xn[ti % 2]
                    S.ts(xnt, xt, stat[:, 1:2], None, op0=ALU.mult)
                    pT = banks[0].cast(BF16)
                    for dc in range(8):
                        S.tr(pT[:, dc * 128:(dc + 1) * 128], xnt[:, dc * 128:(dc + 1) * 128], identb)
                    for dc in range(8):
                        S.act(hTg[:, dc, tl * 128:(tl + 1) * 128], pT[:, dc * 128:(dc + 1) * 128], AF.Identity,
                              scale=modc[:, 0, dc:dc + 1], bias=modc[:, 1, dc:dc + 1])
                if b == DBG_B and gi == 0:
                    dump("hT", hTg.re("p a b -> p (a b)"))
                for tl in range(4):
                    ti = gi * 4 + tl
                    cols = slice(ti * 128, (ti + 1) * 128)
                    Bq, Bk, Br = banks[1], banks[2], banks[3]
                    for (bk, c0, n) in ((Bq, 0, 512), (Bk, 512, 512), (Br, 1024, 280)):
                        for dc in range(8):
                            S.mm(bk[:, 0:n], hTg[:, dc, tl * 128:(tl + 1) * 128], w_in[:, dc, c0:c0 + n],
                                 start=(dc == 0), stop=(dc == 7))
                    S.act(sqscr[:, 0:512], Bq, AF.Square)
                    S.act(sqscr[:, 512:640], Bk[:, 256:384], AF.Square)
                    S.act(sqscr[:, 640:768], Br[:, 0:128], AF.Square)
                    S.reduce(stat[:, 8:20], sqscr[:, 0:768].re("p (h d) -> p h d", d=64))
                    rstd_from(stat[:, 20:32], stat[:, 8:20], 64)
                    S.tt(tmpq, Bq.re("p (h d) -> p h d", d=64), bc_last(stat[:, 20:28], 64), ALU.mult)
                    S.tt(qtok, tmpq, bc_mid(qgs, 8), ALU.mult)
                    pTq = banks[4].cast(BF16)
                    for h in range(8):
                        S.tr(pTq[0:64, h * 128:(h + 1) * 128], qtok[:, h, :], identb)
                    S.copy(qT_all[0:64, :, cols], pTq[0:64, :].re("p (h t) -> p h t", t=128), eng="act")
                    S.tt(tmpk[:, 0:2, :], Bk[:, 256:384].re("p (g d) -> p g d", d=64), bc_last(stat[:, 28:30], 64), ALU.mult)
                    S.tt(ktok[:, 0:2, :], tmpk[:, 0:2, :], bc_mid(C("ksg"), 2), ALU.mult)
                    S.tt(tmpk[:, 2:4, :], Br[:, 0:128].re("p (g d) -> p g d", d=64), bc_last(stat[:, 30:32], 64), ALU.mult)
                    S.tt(ktok[:, 2:4, :], tmpk[:, 2:4, :], bc_mid(C("kwg"), 2), ALU.mult)
                    pTk = banks[5].cast(BF16)
                    for j in range(4):
                        S.tr(pTk[0:64, j * 128:(j + 1) * 128], ktok[:, j, :], identb)
                    S.copy(kT_s[0:64, :, cols], pTk[0:64, 0:256].re("p (g t) -> p g t", t=128))
                    S.copy(kT_w[0:64, :, cols], pTk[0:64, 256:512].re("p (g t) -> p g t", t=128))
                    src = Bk[:, 0:256].re("p (j d) -> p j d", d=64)
                    S.copy(kc2[:, :, 0:64], src, eng="act")
                    S.copy(kc2[:, :, 64:128], src, eng="act")
                    for j in range(4):
                        S.tr(pTk[:, 512 + j * 128:512 + (j + 1) * 128], kc2[:, j, :], identb)
                    pc = pTk[:, 512:1024].re("p (j t) -> p j t", t=128)
                    S.copy(kcT2[0:64, :, cols], pc[0:64])
                    if ti == 0:
                        S.copy(kcT2[64:128, :, 0:127], pc[64:128, :, 1:128])
                    else:
                        S.copy(kcT2[64:128, :, ti * 128 - 1:ti * 128 + 127], pc[64:128])
                    S.copy(vtok_s[:, ti, :, 0:64], Bk[:, 384:512].re("p (g d) -> p g d", d=64), eng="act")
                    S.copy(vtok_w[:, ti, :, 0:64], Br[:, 128:256].re("p (g d) -> p g d", d=64), eng="act")
                    S.act(gtmp, Br[:, 256:280], AF.Exp, scale=-1.0)
                    S.ts(gtmp, gtmp, 1.0, None, op0=ALU.add)
                    S.recip(gates[:, ti, :], gtmp)
                gcols = slice(gi * 512, (gi + 1) * 512)
                for cc in range(4):
                    Cb, Cc, Cx = banks[1], banks[2], banks[3]
                    for (bk, c0) in ((Cb, 1304 + cc * 128), (Cc, 1816 + cc * 128), (Cx, 2328 + cc * 128)):
                        for dc in range(8):
                            S.mm(bk, w_in[:, dc, c0:c0 + 128], hTg[:, dc, :], start=(dc == 0), stop=(dc == 7))
                    ue = uext[cc]
                    cwc = C("cw")[:, cc * 3:(cc + 1) * 3]
                    S.copy(xts, Cx, eng="act")
                    S.tt(ue[:, 2:514], Cc, xts, ALU.mult)
                    S.ts(ctmp, ue[:, 2:514], cwc[:, 2:3], None, op0=ALU.mult, eng="pool")
                    S.ts(ctmp2, ue[:, 1:513], cwc[:, 1:2], None, op0=ALU.mult, eng="pool")
                    S.tt(ctmp, ctmp, ctmp2, ALU.add, eng="pool")
                    S.ts(ctmp2, ue[:, 0:512], cwc[:, 0:1], None, op0=ALU.mult, eng="pool")
                    S.tt(ctmp, ctmp, ctmp2, ALU.add, eng="pool")
                    S.copy(ue[:, 0:2], ue[:, 512:514], eng="pool")
                    S.tt(ybuf[cc], Cb, ctmp, ALU.mult)
                    S.act(ysq, ybuf[cc], AF.Square)
                    S.mm(banks[6], ones_bf, ysq, start=(cc == 0), stop=(cc == 3))
                rstd_from(rcv, banks[6], 512)
                for cc in range(4):
                    S.stt(y_convT[:, cc, gcols], ybuf[cc], C("cog")[:, cc:cc + 1], rcv, ALU.mult, ALU.mult)
            if b == DBG_B:
                dump("qT", qT_all[0:64].re("p a b -> p (a b)"))
                dump("kTs", kT_s[0:96].re("p a b -> p (a b)"))
                dump("ycT", y_convT.re("p a b -> p (a b)"))
                dump("kcT2", kcT2.re("p a b -> p (a b)"))
                dump("gates", gates.re("p a b -> p (a b)"))
            S.barrier()

            S.dma("sp", cpkB, DT(cpk_d[:, NCPKA:NCPK], "in:cpk"), "cpkB")
            S.dma("sp", biasg, DT(biasg_d.rearrange("p (h t) -> p h t", t=256), "in:biasg"), "biasg")
            S.dma("sp", patg, DT(patg_d.rearrange("p (h t) -> p h t", t=512), "in:patg"), "patg")
            for kv, (w1d, w2d) in enumerate(((w_ck1_d, w_ck2_d), (w_cv1_d, w_cv2_d))):
                S.dma("pool", w_c1[kv], DT(wview(w1d), "in:wc1"), "wc1_%d" % kv)
                S.dma("pool", w_c2[kv], DT(wview(w2d), "in:wc2"), "wc2_%d" % kv)
            for g in range(2):
                S.dma("pool", vc_aug[0:127, g, 65:97], DT(ovl_d, "in:ovl"), "ovl%d" % g)
            S.memset(vc_aug[:, :, 64:65], 1.0)
            S.memset(nm96, 0.0)
            S.ts(nb31, C("b31"), -1.0, None, op0=ALU.mult)
            for h in range(8):
                S.act(EWn[:, h, :], biasg[:, h, :], AF.Exp, bias=nb31[:, h:h + 1])
            S.tt(EWn, EWn, bc_mid(C("mask01"), 8), ALU.mult)
            vcd = C("validc")[0:40]
            S.tt(Pat, patg, bc_last(C("b31")[0:40], 512), ALU.subtract)
            S.tt(Pat, Pat, bc_mid(vcd, 8), ALU.mult)
            S.ts(negb, vcd, BIG, -BIG, op0=ALU.mult, op1=ALU.add)
            S.tt(Pat, Pat, bc_mid(negb, 8), ALU.add)
            shm = C("SHM")[0:40]
            S.copy(posb[:, 0, :], C("posk"))
            S.copy(posb[:, 1, :], C("posv"))
            kv2 = kcT2.re("p j (n r) -> p j n r", r=16)
            for kv in range(2):
                for hc in range(2):
                    for i2 in range(16):
                        S.mm(banks[0][:, hc:hc + 1], w_c1[kv][:, i2, hc * 128:(hc + 1) * 128], posb[:, kv, i2:i2 + 1],
                             start=(i2 == 0), stop=(i2 == 15))
                S.copy(posbias[:, kv, :], banks[0][:, 0:2])
            for kv in range(2):
                for g in range(2):
                    j = kv * 2 + g
                    for hc in range(2):
                        bkh = banks[1 + hc]
                        for i2 in range(16):
                            r0 = 2 * i2
                            rhs = kv2[:, j, 0:127, r0] if r0 < 16 else kv2[:, j, 1:128, r0 - 16]
                            S.mm(bkh[:, 0:127], w_c1[kv][:, i2, hc * 128:(hc + 1) * 128], rhs,
                                 start=(i2 == 0), stop=(i2 == 15))
                        S.act(hidT[:, hc, 0:127], bkh[:, 0:127], AF.Silu, bias=posbias[:, kv, hc:hc + 1])
                    bo = banks[3]
                    for hc in range(2):
                        S.mm(bo[0:127, 0:64], hidT[:, hc, 0:127], w_c2[kv][:, hc, :], start=(hc == 0), stop=(hc == 1))
                    if kv == 0:
                        S.act(sqc[0:127, :], bo[0:127, 0:64], AF.Square, accum=stat[0:127, 0:1])
                        rstd_from(stat[0:127, 1:2], stat[0:127, 0:1], 64)
                        S.stt(kcn[0:127, :], bo[0:127, 0:64], stat[0:127, 1:2], C("kcg")[0:127], ALU.mult, ALU.mult)
                        ptk = banks[4].cast(BF16)
                        S.tr(ptk[0:64, 0:127], kcn[0:127, :], identb[0:127, 0:127])
                        S.copy(kcT[0:64, g, 0:127], ptk[0:64, 0:127])
                    else:
                        S.copy(vc_aug[0:127, g, 0:64], bo[0:127, 0:64])
            if b == DBG_B:
                dump("kcT", kcT.re("p a b -> p (a b)"))
                dump("vcaug", vc_aug.re("p a b -> p (a b)"))
                dump("Pat", Pat.re("p a b -> p (a b)"))
                dump("EWn", EWn.re("p a b -> p (a b)"))

            S.barrier()
            S.dma("pool", w_out, DT(wview(w_out_d), "in:w_out"), "wout")
            sbank = [banks[0], banks[1], banks[2]]
            sbi = [0]
            pti = [0]

            def next_sbank():
                bk = sbank[sbi[0] % 3]
                sbi[0] += 1
                return bk

            def next_pt():
                p = PT[pti[0] % 4]
                pti[0] += 1
                return p

            for qg in range(4):
                t0 = qg * 512
                for h in range(8):
                    g = h // 4
                    bs = next_sbank()
                    S.mm(bs[0:127, :], kcT[0:64, g, 0:127], qT_all[0:64, h, t0:t0 + 512], start=True, stop=False)
                    S.mm(bs[0:127, :], shm[:, 96 - 32 * qg:96 - 32 * qg + 127], Pat[:, h, :], start=False, stop=True)
                    pc_ = PcT[h % 2]
                    S.act(pc_[0:127, :], bs[0:127, :], AF.Exp)
                    bo = banks[3].re("p (a b) -> p a b", b=128)
                    for tl in range(4):
                        S.mm(bo[:, tl, 0:97], pc_[0:127, tl * 128:(tl + 1) * 128], vc_aug[0:127, g, :], start=True, stop=True)
                    S.ts(stat[:, 32:36], bo[:, :, 64], 1e-30, None, op0=ALU.add)
                    S.recip(stat[:, 32:36], stat[:, 32:36])
                    S.tt(stat[:, 36:40], stat[:, 32:36], gates[:, 4 * qg:4 * qg + 4, 3 * h], ALU.mult)
                    S.tt(yacc[:, :, h * 64:(h + 1) * 64], bo[:, :, 0:64], bc_last(stat[:, 36:40], 64), ALU.mult)
                    if h % 4 == 0:
                        S.tt(pslc[:, :, g, :], bo[:, :, 65:97], bc_last(stat[:, 32:36], 32), ALU.mult)
                    else:
                        S.tt(ptmp, bo[:, :, 65:97], bc_last(stat[:, 32:36], 32), ALU.mult)
                        S.tt(pslc[:, :, g, :], pslc[:, :, g, :], ptmp, ALU.add)
                if b == DBG_B and qg == 3:
                    dump("pslc", pslc.re("p a b c -> p (a b c)"))
                    dump("yacc_c", yacc.re("p a b -> p (a b)"))
                for tl in range(4):
                    qt = 4 * qg + tl
                    for g in range(2):
                        S.tt(sc_a, pslc[:, tl, g, :], C("M1")[:, qt * 32:(qt + 1) * 32], ALU.mult)
                        S.tt(sc_a, sc_a, C("Aadd")[:, qt * 32:(qt + 1) * 32], ALU.add)
                        S.max8(m8[:, 0:8], sc_a)
                        S.match_replace(sc_b, m8[:, 0:8], sc_a, -1e9)
                        S.max8(m8[:, 8:16], sc_b)
                        S.ts(nm96[:, 64:96], sc_a, m8[:, 15:16], -BIG, op0=ALU.is_lt, op1=ALU.mult)
                        ptn = banks[4].cast(BF16)
                        S.tr(ptn[0:96, 0:128], nm96, identb)
                        S.copy(qT_all[64:96, 4 * g:4 * g + 4, qt * 128:(qt + 1) * 128], bc_mid(ptn[64:96, 0:128], 4))
                for h in range(8):
                    g = h // 4
                    bos = banks[5].re("p (a b) -> p a b", b=128)
                    bow = banks[6].re("p (a b) -> p a b", b=128)
                    for kt in range(4 * qg + 4):
                        n0 = max(0, kt - 4 * qg)
                        N = (4 - n0) * 128
                        bs = next_sbank()
                        S.mm(bs[:, 0:N], kT_s[0:96, g, kt * 128:(kt + 1) * 128], qT_all[0:96, h, t0 + n0 * 128:t0 + 512])
                        pt_ = next_pt()
                        S.act(pt_[:, 0:N], bs[:, 0:N], AF.Exp)
                        lo_q = max(kt, 4 * qg)
                        hi_q = min(kt + 1, 4 * qg + 3)
                        if hi_q >= lo_q:
                            c0 = (lo_q - 4 * qg - n0) * 128
                            w = (hi_q - lo_q + 1) * 128
                            e0 = (lo_q - kt) * 128
                            S.tt(pt_[:, c0:c0 + w], pt_[:, c0:c0 + w], EWn[:, h, e0:e0 + w], ALU.mult,
                                 eng=("pool" if (kt % 2) else "dve"))
                        for tl in range(n0, 4):
                            S.mm(bos[:, tl, 0:65], pt_[:, (tl - n0) * 128:(tl - n0 + 1) * 128], vtok_s[:, kt, g, :],
                                 start=(kt == 0 and tl == 0), stop=(kt == 4 * qg + tl), skip=True)
                    for kt in range(max(0, 4 * qg - 4), 4 * qg + 4):
                        qlo = max(kt, 4 * qg)
                        qhi = min(kt + 4, 4 * qg + 3)
                        N = (qhi - qlo + 1) * 128
                        bs = next_sbank()
                        S.mm(bs[:, 0:N], kT_w[0:64, g, kt * 128:(kt + 1) * 128], qT_all[0:64, h, qlo * 128:(qhi + 1) * 128])
                        pt_ = next_pt()
                        S.act(pt_[:, 0:N], bs[:, 0:N], AF.Exp)
                        lo_q = max(kt, qlo)
                        hi_q = min(kt + 1, qhi)
                        if hi_q >= lo_q:
                            c0 = (lo_q - qlo) * 128
                            w = (hi_q - lo_q + 1) * 128
                            e0 = (lo_q - kt) * 128
                            S.tt(pt_[:, c0:c0 + w], pt_[:, c0:c0 + w], EWn[:, h, e0:e0 + w], ALU.mult,
                                 eng=("pool" if (kt % 2) else "dve"))
                        if qlo <= kt + 4 <= qhi:
                            c0 = (kt + 4 - qlo) * 128
                            S.tt(pt_[:, c0:c0 + 128], pt_[:, c0:c0 + 128], C("mfar"), ALU.mult,
                                 eng=("dve" if (kt % 2) else "pool"))
                        for qt in range(qlo, qhi + 1):
                            S.mm(bow[:, qt - 4 * qg, 0:65], pt_[:, (qt - qlo) * 128:(qt - qlo + 1) * 128], vtok_w[:, kt, g, :],
                                 start=(kt == max(0, 4 * qg - 4) and qt == qlo), stop=(kt == qt), skip=True)
                    for (bo_, br) in ((bos, 1), (bow, 2)):
                        S.ts(stat[:, 40:44], bo_[:, :, 64], 1e-30, None, op0=ALU.add)
                        S.recip(stat[:, 40:44], stat[:, 40:44])
                        S.tt(stat[:, 44:48], stat[:, 40:44], gates[:, 4 * qg:4 * qg + 4, 3 * h + br], ALU.mult)
                        S.tt(otmp, bo_[:, :, 0:64], bc_last(stat[:, 44:48], 64), ALU.mult)
                        S.tt(yacc[:, :, h * 64:(h + 1) * 64], yacc[:, :, h * 64:(h + 1) * 64], otmp, ALU.add, eng="pool")
                if b == DBG_B and qg == 3:
                    dump("yacc", yacc.re("p a b -> p (a b)"))
                for tl in range(4):
                    S.act(ysq2, yacc[:, tl, :], AF.Square, accum=stat[:, 48 + tl:49 + tl])
                rstd_from(stat[:, 52:56], stat[:, 48:52], 512)
                for tl in range(4):
                    S.stt(yn[:, tl, :], yacc[:, tl, :], stat[:, 52 + tl:53 + tl], C("aog"), ALU.mult, ALU.mult)
                    pty = banks[7].cast(BF16)
                    for c in range(4):
                        S.tr(pty[:, c * 128:(c + 1) * 128], yn[:, tl, c * 128:(c + 1) * 128], identb)
                    S.copy(y_attnT[:, :, t0 + tl * 128:t0 + (tl + 1) * 128], pty[:, 0:512].re("p (c t) -> p c t", t=128),
                           eng="act")
            if b == DBG_B:
                dump("yaT", y_attnT.re("p a b -> p (a b)"))
            S.barrier()

            S.dma("pool", w_ff1, DT(wview(w_ff1_d), "in:w_ff1"), "wff1")
            S.dma("pool", w_ff3, DT(wview(w_ff3_d), "in:w_ff3"), "wff3")
            S.dma("pool", w_ff2, DT(wview(w_ff2_d), "in:w_ff2"), "wff2")
            S.dma("sp", gbc, DT(mod_d[b:b + 1, 2 * DM:3 * DM].partition_broadcast(128), "dram:mod"), "gbc")
            for ti in range(NT):
                cols = slice(ti * 128, (ti + 1) * 128)
                xt = xs[ti % 2]
                S.dma("sp", xt, DT(x_d[b, cols, :], "in:x"), "x%d" % (ti % 2))
                xo = x1t[ti % 2]
                for half in range(2):
                    bk = banks[(ti * 2 + half) % 4]
                    for kc in range(8):
                        src = y_attnT if kc < 4 else y_convT
                        S.mm(bk, src[:, kc % 4, cols], w_out[:, kc, half * 512:(half + 1) * 512],
                             start=(kc == 0), stop=(kc == 7))
                    hs = slice(half * 512, (half + 1) * 512)
                    S.tt(xo[:, hs], bk, gbc[:, hs], ALU.mult)
                    S.tt(xo[:, hs], xo[:, hs], xt[:, hs], ALU.add, eng="pool")
                S.dma("sp", DT(x1_d[b, cols, :], "dram:x1:%d:%d" % (b, ti)), xo, "x1st%d" % (ti % 2))
            S.barrier()

            S.dma("sp", modl[:, 0, :], DT(mod_d[b, 4 * DM:5 * DM].rearrange("(c p) -> p c", p=128), "dram:mod"), "modl0")
            S.dma("sp", modl[:, 1, :], DT(mod_d[b, 3 * DM:4 * DM].rearrange("(c p) -> p c", p=128), "dram:mod"), "modl1")
            S.stt(modc[:, 2, :], modl[:, 0, :], 1.0, g2c, ALU.add, ALU.mult)
            S.copy(modc[:, 3, :], modl[:, 1, :])
            S.dma("sp", gbc, DT(mod_d[b:b + 1, 5 * DM:6 * DM].partition_broadcast(128), "dram:mod"), "gbc")
            for gi in range(4):
                for tl in range(4):
                    ti = gi * 4 + tl
                    xt = x1g[:, tl, :].k("x1g%d" % tl)
                    S.dma("sp", xt, DT(x1_d[b, ti * 128:(ti + 1) * 128, :], "dram:x1:%d:%d" % (b, ti)), "x1ld%d" % tl)
                    S.act(sq3, xt, AF.Square, accum=stat[:, 0:1])
                    rstd_from(stat[:, 1:2], stat[:, 0:1], DM)
                    xnt = xn3[ti % 2]
                    S.ts(xnt, xt, stat[:, 1:2], None, op0=ALU.mult)
                    pT = banks[0].cast(BF16)
                    for dc in range(8):
                        S.tr(pT[:, dc * 128:(dc + 1) * 128], xnt[:, dc * 128:(dc + 1) * 128], identb)
                    for dc in range(8):
                        S.act(h2T[:, dc, tl * 128:(tl + 1) * 128], pT[:, dc * 128:(dc + 1) * 128], AF.Identity,
                              scale=modc[:, 2, dc:dc + 1], bias=modc[:, 3, dc:dc + 1])
                for f in range(NF):
                    b1 = banks[1 + (f % 2) * 2]
                    b3 = banks[2 + (f % 2) * 2]
                    for dc in range(8):
                        S.mm(b1, w_ff1[:, dc, f * 128:(f + 1) * 128], h2T[:, dc, :], start=(dc == 0), stop=(dc == 7))
                    for dc in range(8):
                        S.mm(b3, w_ff3[:, dc, f * 128:(f + 1) * 128], h2T[:, dc, :], start=(dc == 0), stop=(dc == 7))
                    S.act(sg, b1, AF.Silu)
                    S.tt(uT[:, f, :], b3, sg, ALU.mult)
                for tl in range(4):
                    ti = gi * 4 + tl
                    oo = ot[ti % 2]
                    for half in range(2):
                        bk = banks[5 + half]
                        for f in range(NF):
                            S.mm(bk, uT[:, f, tl * 128:(tl + 1) * 128], w_ff2[:, f, half * 512:(half + 1) * 512],
                                 start=(f == 0), stop=(f == NF - 1))
                        hs = slice(half * 512, (half + 1) * 512)
                        S.tt(oo[:, hs], bk, gbc[:, hs], ALU.mult)
                        S.tt(oo[:, hs], oo[:, hs], x1g[:, tl, hs].k("x1g%d" % tl), ALU.add, eng="pool")
                    S.dma("sp", DT(out_d[b, ti * 128:(ti + 1) * 128, :], "out:%d:%d" % (b, ti)), oo, "ost%d" % (ti % 2))
            S.barrier()
        S.wait_all_dma("sp")
        S.finalize()
        S.replay()
    return nc


_WNAMES = ("w_ada", "w_in", "w_ck1", "w_ck2", "w_cv1", "w_cv2", "w_out", "w_ff1", "w_ff3", "w_ff2")


def _run(inputs, dbg=None, cores=8):
    inp = {k: np.asarray(v) for k, v in inputs.items()}
    nc = build_program(dbg)
    shared = {n: np.ascontiguousarray(inp[n][0], dtype=np.float32) for n in _WNAMES}
    in_maps = []
    for c in range(cores):
        m = dict(shared)
        m["x"] = np.ascontiguousarray(inp["x"][2 * c:2 * c + 2], dtype=np.float32)
        m.update(_host_constants(inp, c))
        in_maps.append(m)
    res = run_bass_kernel_spmd(nc, in_maps, core_ids=list(range(cores)))
    return res


def kernel(**inputs):
    res = _run(inputs)
    return np.concatenate([r["out"] for r in res.results], axis=0).astype(np.float32)
```
